# Optimizing a Trainium2 kernel written in Bass

```python
import jax, jax.numpy as jnp
from jax import lax
import numpy as np

D_MODEL = 1024
BATCH = 8
SEQ = 4096
DEPTH = 4

N_META = 16
EPS = 1e-6
D_FF = 2816
FFN_HALF = 0.5

A_WIDTH = 3 * D_MODEL // 8
A_DK = 64
A_DV = 64
A_HEADS = A_WIDTH // A_DV
A_CHUNK = 64

B_HEAD_DIM = 64
B_WIDTH = 3 * D_MODEL // 8
B_Q_HEADS = B_WIDTH // B_HEAD_DIM
B_KV_HEADS = 2
B_WINDOW = 128
B_BLOCK = 128

C_CHANNELS = D_MODEL - A_WIDTH - B_WIDTH
C_CONV_WIDTH = 31

MIX_WIDTH = A_WIDTH + B_WIDTH + C_CHANNELS
IN_SPLITS = (A_HEADS * A_DK, A_HEADS * A_DK, A_WIDTH, A_WIDTH,
             B_Q_HEADS * B_HEAD_DIM, B_KV_HEADS * B_HEAD_DIM, B_KV_HEADS * B_HEAD_DIM,
             2 * C_CHANNELS)
IN_WIDTH = sum(IN_SPLITS)

kernel_name = 'hymba_style_hgrn2_swa_conformer_macaron'


def rms_norm(x, g):
    xf = x.astype(jnp.float32)
    y = xf * lax.rsqrt(jnp.mean(xf * xf, axis=-1, keepdims=True) + EPS)
    return (y * g.astype(jnp.float32)).astype(x.dtype)


def swiglu(x, w_gate, w_up, w_down):
    return (jax.nn.silu(x @ w_gate) * (x @ w_up)) @ w_down


def pad_front(t, n):
    return jnp.pad(t, ((0, 0), (n, 0), (0, 0)))


def hgrn2_recurrence(q, f_pre, i, g, lb, out_gain):
    Bsz, L, _ = q.shape
    dt = q.dtype
    pad = A_CHUNK - N_META
    total = L + pad
    nc = total // A_CHUNK
    z = f_pre.astype(jnp.float32)
    lb = lb.astype(jnp.float32)
    log_f = jnp.logaddexp(jnp.log(lb), jnp.log1p(-lb) + jax.nn.log_sigmoid(z))
    k = (1.0 - lb) * jax.nn.sigmoid(-z)

    def chunks(t, d):
        return pad_front(t, pad).reshape(Bsz, nc, A_CHUNK, A_HEADS, d).transpose(1, 0, 3, 2, 4)

    qc = chunks(q.astype(jnp.float32), A_DK)
    kc = chunks(k, A_DK)
    gc = chunks(log_f, A_DK)
    vc = chunks(i.astype(jnp.float32), A_DV)
    causal = jnp.tril(jnp.ones((A_CHUNK, A_CHUNK), dtype=bool))

    def step(S, inp):
        qb, kb, vb, gb = inp
        G = jnp.cumsum(gb, axis=2)
        o_inter = jnp.einsum('bhtk,bhkv->bhtv', qb * jnp.exp(G), S)
        diff = G[:, :, :, None, :] - G[:, :, None, :, :]
        decay = jnp.exp(jnp.where(causal[:, :, None], diff, -jnp.inf))
        A = jnp.einsum('bhtsk,bhsk->bhts', decay * qb[:, :, :, None, :], kb)
        o_intra = jnp.einsum('bhts,bhsv->bhtv', A, vb)
        G_last = G[:, :, -1:, :]
        S_new = (jnp.exp(G_last[:, :, 0, :])[..., None] * S
                 + jnp.einsum('bhsk,bhsv->bhkv', kb * jnp.exp(G_last - G), vb))
        return S_new, o_inter + o_intra

    S0 = jnp.zeros((Bsz, A_HEADS, A_DK, A_DV), jnp.float32)
    _, o = lax.scan(step, S0, (qc, kc, vc, gc))
    o = o.transpose(1, 0, 3, 2, 4).reshape(Bsz, total, A_HEADS, A_DV)[:, pad:]
    o = rms_norm(o, out_gain)
    gate = jax.nn.silu(g.astype(jnp.float32)).reshape(Bsz, L, A_HEADS, A_DV)
    return (o * gate).reshape(Bsz, L, A_WIDTH).astype(dt)


def sliding_window_sink_attention(q, k, v, sinks):
    Bsz, L, _ = q.shape
    dt = q.dtype
    G = B_Q_HEADS // B_KV_HEADS
    pad = B_BLOCK - N_META
    total = L + pad
    nb = total // B_BLOCK
    qb = pad_front(q, pad).reshape(Bsz, nb, B_BLOCK, B_KV_HEADS, G, B_HEAD_DIM)
    kb = pad_front(k, pad).reshape(Bsz, nb, B_BLOCK, B_KV_HEADS, B_HEAD_DIM)
    vb = pad_front(v, pad).reshape(Bsz, nb, B_BLOCK, B_KV_HEADS, B_HEAD_DIM)

    def band(t):
        prev = jnp.concatenate([jnp.zeros_like(t[:, :1]), t[:, :-1]], axis=1)
        return jnp.concatenate([prev, t], axis=2)

    k_band, v_band = band(kb), band(vb)
    k_meta = k[:, :N_META].reshape(Bsz, N_META, B_KV_HEADS, B_HEAD_DIM)
    v_meta = v[:, :N_META].reshape(Bsz, N_META, B_KV_HEADS, B_HEAD_DIM)
    scale = B_HEAD_DIM ** -0.5
    s_band = jnp.einsum('bnqhgd,bnkhd->bnhgqk', qb, k_band)
    s_meta = jnp.einsum('bnqhgd,bmhd->bnhgqm', qb, k_meta)
    scores = jnp.concatenate([s_band, s_meta], axis=-1).astype(jnp.float32) * scale

    q_pos = jnp.arange(total).reshape(nb, B_BLOCK)
    k_pos = (jnp.arange(nb)[:, None] - 1) * B_BLOCK + jnp.arange(2 * B_BLOCK)[None, :]
    qp, kp = q_pos[:, :, None], k_pos[:, None, :]
    band_mask = (kp >= pad + N_META) & (kp <= qp) & (qp - kp < B_WINDOW)
    meta_mask = (pad + jnp.arange(N_META))[None, None, :] <= qp
    mask = jnp.concatenate([band_mask, meta_mask], axis=-1)[None, :, None, None]
    scores = jnp.where(mask, scores, -jnp.inf)
    sink = jnp.broadcast_to(sinks.astype(jnp.float32).reshape(1, 1, B_KV_HEADS, G, 1, 1),
                            scores.shape[:-1] + (1,))
    probs = jax.nn.softmax(jnp.concatenate([scores, sink], axis=-1), axis=-1)[..., :-1].astype(dt)
    out = (jnp.einsum('bnhgqk,bnkhd->bnqhgd', probs[..., :2 * B_BLOCK], v_band)
           + jnp.einsum('bnhgqm,bmhd->bnqhgd', probs[..., 2 * B_BLOCK:], v_meta))
    return out.reshape(Bsz, total, B_Q_HEADS * B_HEAD_DIM)[:, pad:]


def conformer_conv(u, dw_w, dw_b, ln_g, ln_b):
    a, b = jnp.split(u, 2, axis=-1)
    h = a * jax.nn.sigmoid(b)
    h = lax.conv_general_dilated(h, dw_w[:, None, :].astype(h.dtype), (1,),
                                 [(C_CONV_WIDTH - 1, 0)],
                                 dimension_numbers=('NWC', 'WIO', 'NWC'),
                                 feature_group_count=C_CHANNELS) + dw_b
    hf = h.astype(jnp.float32)
    mu = jnp.mean(hf, axis=-1, keepdims=True)
    var = jnp.mean(jnp.square(hf - mu), axis=-1, keepdims=True)
    hn = (hf - mu) * lax.rsqrt(var + EPS) * ln_g.astype(jnp.float32) + ln_b.astype(jnp.float32)
    return jax.nn.silu(hn).astype(u.dtype)


def setup_inputs(seed: int = 0) -> dict:
    key = jax.random.key(seed)
    ks = jax.random.split(key, 24)
    f32 = jnp.float32

    def nrm(k, shape, scale):
        return jax.random.normal(k, shape, f32) * scale

    def gain(k, shape):
        return 1.0 + 0.02 * jax.random.normal(k, shape, f32)

    return {
        'x': nrm(ks[0], (BATCH, SEQ, D_MODEL), 1.0),
        'meta_tokens': nrm(ks[1], (N_META, D_MODEL), 1.0),
        'ffn1_norm': gain(ks[2], (DEPTH, D_MODEL)),
        'ffn1_w_gate': nrm(ks[3], (DEPTH, D_MODEL, D_FF), D_MODEL ** -0.5),
        'ffn1_w_up': nrm(ks[4], (DEPTH, D_MODEL, D_FF), D_MODEL ** -0.5),
        'ffn1_w_down': nrm(ks[5], (DEPTH, D_FF, D_MODEL), D_FF ** -0.5),
        'mix_norm': gain(ks[6], (DEPTH, D_MODEL)),
        'w_in': nrm(ks[7], (DEPTH, D_MODEL, IN_WIDTH), D_MODEL ** -0.5),
        'w_out': nrm(ks[8], (DEPTH, MIX_WIDTH, D_MODEL), MIX_WIDTH ** -0.5),
        'hgrn_lb_logits': nrm(ks[9], (DEPTH, A_HEADS * A_DK), 0.5),
        'hgrn_out_norm': gain(ks[10], (DEPTH, A_DV)),
        'attn_sinks': nrm(ks[11], (DEPTH, B_Q_HEADS), 0.5),
        'conv_dw_w': nrm(ks[12], (DEPTH, C_CONV_WIDTH, C_CHANNELS), C_CONV_WIDTH ** -0.5),
        'conv_dw_b': nrm(ks[13], (DEPTH, C_CHANNELS), 0.02),
        'conv_ln_g': gain(ks[14], (DEPTH, C_CHANNELS)),
        'conv_ln_b': nrm(ks[15], (DEPTH, C_CHANNELS), 0.02),
        'ffn2_norm': gain(ks[16], (DEPTH, D_MODEL)),
        'ffn2_w_gate': nrm(ks[17], (DEPTH, D_MODEL, D_FF), D_MODEL ** -0.5),
        'ffn2_w_up': nrm(ks[18], (DEPTH, D_MODEL, D_FF), D_MODEL ** -0.5),
        'ffn2_w_down': nrm(ks[19], (DEPTH, D_FF, D_MODEL), D_FF ** -0.5),
        'final_norm': gain(ks[20], (D_MODEL,)),
    }


def reference(x, meta_tokens, ffn1_norm, ffn1_w_gate, ffn1_w_up, ffn1_w_down,
              mix_norm, w_in, w_out, hgrn_lb_logits, hgrn_out_norm, attn_sinks,
              conv_dw_w, conv_dw_b, conv_ln_g, conv_ln_b,
              ffn2_norm, ffn2_w_gate, ffn2_w_up, ffn2_w_down, final_norm):
    Bsz = x.shape[0]
    meta = jnp.broadcast_to(meta_tokens[None].astype(x.dtype), (Bsz, N_META, D_MODEL))
    h = jnp.concatenate([meta, x], axis=1)
    lbs = jnp.cumsum(jax.nn.softmax(hgrn_lb_logits.astype(jnp.float32), axis=0), axis=0)
    lbs = lbs - lbs[0]
    split_points = np.cumsum(IN_SPLITS)[:-1].tolist()
    for l in range(DEPTH):
        h = h + FFN_HALF * swiglu(rms_norm(h, ffn1_norm[l]), ffn1_w_gate[l], ffn1_w_up[l], ffn1_w_down[l])
        p = rms_norm(h, mix_norm[l]) @ w_in[l]
        a_q, a_f, a_i, a_g, b_q, b_k, b_v, c_u = jnp.split(p, split_points, axis=-1)
        y_a = hgrn2_recurrence(a_q, a_f, a_i, a_g, lbs[l], hgrn_out_norm[l])
        y_b = sliding_window_sink_attention(b_q, b_k, b_v, attn_sinks[l])
        y_c = conformer_conv(c_u, conv_dw_w[l], conv_dw_b[l], conv_ln_g[l], conv_ln_b[l])
        h = h + jnp.concatenate([y_a, y_b, y_c], axis=-1) @ w_out[l]
        h = h + FFN_HALF * swiglu(rms_norm(h, ffn2_norm[l]), ffn2_w_gate[l], ffn2_w_up[l], ffn2_w_down[l])
    return rms_norm(h[:, N_META:], final_norm)
```

```python
import numpy as np
from contextlib import ExitStack
import concourse.bass as bass
import concourse.mybir as mybir
from concourse.bass_utils import run_bass_kernel_spmd

F32 = mybir.dt.float32
BF16 = mybir.dt.bfloat16
AF = mybir.ActivationFunctionType
ALU = mybir.AluOpType
AX = mybir.AxisListType

EPS = 1e-6
D = 1024
DFF = 2816
NF = 22
NMETA = 16
TS = 512
TM = 256
AQ, AFo, AI, AG, BQ, BK, BV, CU = 0, 384, 768, 1152, 1536, 1920, 2048, 2176
NEG = -30000.0
ENGS = ('pe', 'act', 'dve', 'pool', 'sp')
SAME_ENGINE_SYNC = True


class Sched:
    def __init__(self, nc, es):
        self.nc = nc
        self.es = es
        self.q = {e: [] for e in ENGS}
        self.sems = {}
        self.cnt = {}
        self.known = {e: {} for e in ENGS}
        self.lastw = {}
        self.readers = {}
        for e in ENGS:
            self.newsem('E_' + e)

    def newsem(self, name):
        if name not in self.sems:
            self.sems[name] = self.es.enter_context(self.nc.semaphore(name))
            self.cnt[name] = 0
        return name

    def _deps(self, reads, writes):
        d = []
        for k in reads:
            ev = self.lastw.get(k)
            if ev is not None:
                d.append(ev)
        for k in writes:
            ev = self.lastw.get(k)
            if ev is not None:
                d.append(ev)
            r = self.readers.get(k)
            if r:
                d.extend(r.items())
        return d

    def _waits(self, eng, deps, own_sync):
        need = {}
        own = 'E_' + eng
        kn = self.known[eng]
        for (s, v) in deps:
            if s == own and not own_sync:
                continue
            if kn.get(s, 0) >= v:
                continue
            if need.get(s, 0) < v:
                need[s] = v
        for s, v in need.items():
            self.q[eng].append(('w', s, v))
            kn[s] = v

    def _record(self, ev, reads, writes):
        s, v = ev
        for k in reads:
            r = self.readers.setdefault(k, {})
            if r.get(s, 0) < v:
                r[s] = v
        for k in writes:
            self.lastw[k] = ev
            self.readers[k] = {}

    def op(self, eng, fn, reads=(), writes=()):
        deps = self._deps(reads, writes)
        self._waits(eng, deps, own_sync=(eng != 'pe') and SAME_ENGINE_SYNC)
        s = 'E_' + eng
        self.cnt[s] += 1
        ev = (s, self.cnt[s])
        self.q[eng].append(('o', fn, s, 1))
        self._record(ev, reads, writes)
        return ev

    def dma(self, eng, sem, fn, reads=(), writes=()):
        self.newsem(sem)
        deps = self._deps(reads, writes)
        self._waits(eng, deps, own_sync=True)
        self.cnt[sem] += 16
        ev = (sem, self.cnt[sem])
        self.q[eng].append(('o', fn, sem, 16))
        self._record(ev, reads, writes)
        return ev

    def wait_events(self, eng, evs):
        self._waits(eng, evs, own_sync=True)

    def emit(self):
        nc = self.nc
        sems = self.sems
        q = self.q

        def run(name):
            def f(eng):
                for it in q[name]:
                    if it[0] == 'w':
                        eng.wait_ge(sems[it[1]], it[2])
                    else:
                        ins = it[1](eng)
                        ins.then_inc(sems[it[2]], it[3])
            return f

        with nc.Block() as block:
            block.tensor(run('pe'))
            block.scalar(run('act'))
            block.vector(run('dve'))
            block.gpsimd(run('pool'))
            block.sync(run('sp'))


class Stream:
    def __init__(self, S, name, slots, pieces, depth):
        self.S = S
        self.name = name
        self.slots = slots
        self.pieces = pieces
        self.issued = 0
        self.used = 0
        self.depth = depth

    def _issue(self):
        i = self.issued
        if i >= len(self.pieces):
            return
        slot = i % len(self.slots)
        src, rkeys = self.pieces[i]
        dst = self.slots[slot]
        key = '%s_s%d' % (self.name, slot)
        self.S.dma('sp', 'ld_' + key, lambda e, d=dst, s=src: e.dma_start(out=d[:], in_=s),
                   reads=rkeys, writes=[key])
        self.issued += 1

    def prime(self):
        while self.issued < min(self.depth, len(self.pieces)):
            self._issue()

    def get(self):
        i = self.used
        while self.issued <= i:
            self._issue()
        slot = i % len(self.slots)
        self.used += 1
        return self.slots[slot], '%s_s%d' % (self.name, slot)

    def after_use(self):
        while self.issued < min(self.used + self.depth, len(self.pieces)):
            self._issue()


def build(L, NT, nmeta_tile=True):
    nc = bass.Bass("TRN2", target_bir_lowering=False)
    es = ExitStack()
    with es:
        S = Sched(nc, es)

        def din(name, shape):
            return nc.dram_tensor(name, shape, F32, kind="ExternalInput").ap()

        def dscr(name, shape):
            return nc.dram_tensor(name, shape, BF16, kind="Internal").ap()

        def sb(name, shape, dt):
            return es.enter_context(nc.sbuf_tensor("sb_" + name, shape, dt))

        xT = din("xT", [NT, 128, 8, TS])
        metaT = din("metaT", [128, 8, NMETA])
        wgu_d = [din("wgu%d" % i, [L, NF, 128, 2, 8, 128]) for i in (1, 2)]
        wd_d = [din("wd%d" % i, [L, 8, 128, NF, 128]) for i in (1, 2)]
        win_d = din("win", [L, 7, 128, 8, 384])
        wo_d = din("wo", [L, 8, 128, 14, 128])
        norms_d = din("norms", [128, L, 3, 8])
        fnorm_d = din("fnorm", [128, 8])
        lbl_d = din("lbl", [64, 6, L])
        ogn_d = din("ogn", [64, L])
        sinks_d = din("sinks", [64, L, 6])
        cw_d = din("cw", [128, L, 2, 31])
        cp_d = din("cp", [128, L, 3, 2])
        yT = nc.dram_tensor("yT", [NT, 128, 8, TS], F32, kind="ExternalOutput").ap()

        wgu_s = [dscr("wgu%ds" % i, [L, NF, 128, 2, 8, 128]) for i in (1, 2)]
        wd_s = [dscr("wd%ds" % i, [L, 8, 128, NF, 128]) for i in (1, 2)]
        win_s = dscr("wins", [L, 7, 128, 8, 384])
        wo_s = dscr("wos", [L, 8, 128, 14, 128])

        psb = [es.enter_context(nc.psum_tensor("ps%d" % i, [128, 512], F32)) for i in range(8)]
        ps_i = [0]

        def PS():
            i = ps_i[0]
            ps_i[0] = (i + 1) % 8
            return psb[i], 'ps%d' % i

        hT = sb("hT", [128, 8, TS], F32)
        obuf = [sb("obuf%d" % i, [128, TS], F32) for i in range(2)]
        xn = sb("xn", [128, 8, TS], BF16)
        rstd = sb("rstd", [128, TS], F32)
        act = sb("act", [128, NF, TS], BF16)
        sg = [sb("sg%d" % i, [128, TS], F32) for i in range(2)]
        wgu_slots = [sb("wgus%d" % i, [128, 2, 8, 128], BF16) for i in range(2)]
        wd_slots = [sb("wds%d" % i, [128, NF, 128], BF16) for i in range(2)]
        win_slots = [sb("wins%d" % i, [128, 8, 384], BF16) for i in range(3)]
        wo_slots = [sb("wos%d" % i, [128, 14, 128], BF16) for i in range(2)]
        cbias = sb("cbias", [128, 4], F32)
        ones_bf = sb("ones_bf", [128, 128], BF16)
        onesf = sb("onesf", [128, TS], F32)
        zerof = sb("zerof", [128, 384], F32)
        ident_f = sb("ident_f", [128, 128], F32)
        ident_bf = sb("ident_bf", [128, 128], BF16)
        maskb_cur = sb("maskb_cur", [128, 3, 128], BF16)
        maskb_prev = sb("maskb_prev", [128, 3, 128], BF16)
        mask_ut = sb("mask_ut", [64, 6, 64], F32)
        norms = sb("norms", [128, L, 3, 8], F32)
        fnorm = sb("fnorm", [128, 8], F32)
        lbl = sb("lbl", [64, 6, L], F32)
        lbe = sb("lbe", [64, 6, L], F32)
        lbs = sb("lbs", [64, 6], F32)
        lb = sb("lb", [64, 6, L], F32)
        oml = sb("oml", [64, 6, L], F32)
        noml = sb("noml", [64, 6, L], F32)
        ogn = sb("ogn", [64, L], F32)
        esink = sb("esink", [64, L, 6], F32)
        cw = sb("cw", [128, L, 2, 31], F32)
        cpar = sb("cpar", [128, L, 3, 2], F32)
        bufA = sb("bufA", [64, 6, 4, 64], F32)
        bufB = sb("bufB", [64, 6, 4, 64], F32)
        bufC = sb("bufC", [64, 6, 4, 64], F32)
        osb = sb("osb", [64, 6, 4, 64], F32)
        qt = sb("qt", [64, 6, 4, 64], BF16)
        kt = sb("kt", [64, 6, 4, 64], BF16)
        sgate = sb("sgate", [64, 6, 4, 64], BF16)
        sqa = sb("sqa", [64, 6, 4, 64], BF16)
        ya = sb("ya", [64, 6, 4, 64], BF16)
        vtokA = sb("vtokA", [64, 4, 384], BF16)
        at_sb = [sb("at_sb%d" % i, [64, 384], BF16) for i in range(2)]
        ktok = [sb("ktok%d" % i, [64, 384], BF16) for i in range(2)]
        t1 = sb("t1", [64, 6, 64], F32)
        Sr = sb("Sr", [64, 6, 64], BF16)
        Sst = [sb("Sst%d" % l, [64, 6, 64], F32) for l in range(L)]
        GS = sb("GS", [64, 6, 4], F32)
        dd = sb("dd", [64, 6, 3, 4], F32)
        ESC = sb("ESC", [64, 6, 3, 4], F32)
        rsa = [sb("rsa%d" % i, [64, TM], F32) for i in range(2)]
        qb = sb("qb", [64, 6, TM], BF16)
        kext = [sb("kext%d" % l, [64, 2, 128 + TM], BF16) for l in range(L)]
        vext = [sb("vext%d" % l, [128, 1 + TM // 128, 128], BF16) for l in range(L)]
        kmeta = [sb("kmeta%d" % l, [64, 2, NMETA], BF16) for l in range(L)]
        vmeta = [sb("vmeta%d" % l, [NMETA, 128], BF16) for l in range(L)]
        pcur = [sb("pcur%d" % i, [128, 384], BF16) for i in range(2)]
        pprev = [sb("pprev%d" % i, [128, 384], BF16) for i in range(2)]
        pmeta = [sb("pmeta%d" % i, [NMETA, 384], BF16) for i in range(2)]
        yb = sb("yb", [64, 6, TM], BF16)
        den = sb("den", [64, 3, 128], F32)
        rden = sb("rden", [64, 3, 128], F32)
        glu = [sb("glu%d" % l, [128, 2, 30 + TM], BF16) for l in range(L)]
        cacc = sb("cacc", [128, 2, TM], F32)
        cvb = sb("cvb", [128, 2, TM], BF16)
        sqb = sb("sqb", [128, 2, TM], BF16)
        cm = sb("cm", [128, TM], F32)
        cmsq = sb("cmsq", [128, TM], F32)
        cvar = sb("cvar", [128, TM], F32)
        crs = sb("crs", [128, TM], F32)
        dcen = [sb("dcen%d" % i, [128, TM], F32) for i in range(2)]
        yc = sb("yc", [128, 2, TM], BF16)
        sgm = [sb("sgm%d" % i, [128, TM], F32) for i in range(2)]

        HK = ['h:%d' % d for d in range(8)]
        ACTK = ['act:%d' % f for f in range(NF)]

        def mm_group(out_ap, pairs, reads, writes):
            n = len(pairs)

            def fn(pe):
                ins = None
                for i, (a, b) in enumerate(pairs):
                    ins = pe.matmul(out_ap, a, b, start=(i == 0), stop=(i == n - 1))
                return ins
            S.op('pe', fn, reads, writes)

        def mm_multi(groups, reads, writes):
            def fn(pe):
                ins = None
                for out_ap, pairs in groups:
                    n = len(pairs)
                    for i, (a, b) in enumerate(pairs):
                        ins = pe.matmul(out_ap, a, b, start=(i == 0), stop=(i == n - 1))
                return ins
            S.op('pe', fn, reads, writes)

        def act_op(out, in_, func, reads, writes, bias=None, scale=None):
            kw = {}
            if bias is not None:
                kw['bias'] = bias
            if scale is not None:
                kw['scale'] = scale
            S.op('act', lambda e: e.activation(out, in_, func, **kw), reads, writes)

        def tt(eng, out, in0, in1, op, reads, writes):
            S.op(eng, lambda e: e.tensor_tensor(out, in0, in1, op), reads, writes)

        def ts(eng, out, in0, s1, s2, op0, op1, reads, writes):
            if op1 is None:
                S.op(eng, lambda e: e.tensor_scalar(out, in0, s1, None, op0), reads, writes)
            else:
                S.op(eng, lambda e: e.tensor_scalar(out, in0, s1, s2, op0, op1), reads, writes)

        def stt(eng, out, in0, scalar, in1, op0, op1, reads, writes):
            eng = 'dve'
            S.op(eng, lambda e: e.scalar_tensor_tensor(out, in0, scalar, in1, op0, op1), reads, writes)

        def rsq(out, in_, c, reads, writes):
            col = {1024.0 * EPS: 0, 64.0 * EPS: 1, EPS: 2}[c]
            npart = out.shape[0]
            act_op(out, in_, AF.Sqrt, list(reads) + ['cbias'], writes, bias=cbias[0:npart, col:col + 1])
            S.op('dve', lambda e: e.reciprocal(out, out), writes, writes)

        def cp(eng, out, in_, reads, writes):
            if eng == 'act':
                S.op(eng, lambda e: e.activation(out, in_, AF.Copy), reads, writes)
            else:
                S.op(eng, lambda e: e.tensor_copy(out, in_), reads, writes)

        S.op('pool', lambda e: e.memset(cbias[:, 0:1], 1024.0 * EPS), writes=['cbias'])
        S.op('pool', lambda e: e.memset(cbias[:, 1:2], 64.0 * EPS), reads=['cbias'], writes=['cbias'])
        S.op('pool', lambda e: e.memset(cbias[:, 2:3], EPS), reads=['cbias'], writes=['cbias'])
        S.op('pool', lambda e: e.memset(ones_bf[:], 1.0), writes=['ones_bf'])
        S.op('pool', lambda e: e.memset(onesf[:], 1.0), writes=['onesf'])
        S.op('pool', lambda e: e.memset(zerof[:], 0.0), writes=['zerof'])
        S.op('pool', lambda e: e.memset(ident_f[:], 1.0), writes=['ident_f'])
        S.op('pool', lambda e: e.affine_select(ident_f[:], ident_f[:], [[-1, 128]], ALU.is_equal, 0.0,
                                               base=0, channel_multiplier=1),
             reads=['ident_f'], writes=['ident_f'])
        cp('pool', ident_bf[:], ident_f[:], ['ident_f'], ['ident_bf'])
        S.op('pool', lambda e: e.affine_select(maskb_cur[:], zerof[:, :].rearrange("p (h q) -> p h q", q=128),
                                               [[0, 3], [1, 128]], ALU.is_ge, NEG, base=0, channel_multiplier=-1),
             reads=['zerof'], writes=['maskb_cur'])
        S.op('pool', lambda e: e.affine_select(maskb_prev[:], zerof[:, :].rearrange("p (h q) -> p h q", q=128),
                                               [[0, 3], [-1, 128]], ALU.is_gt, NEG, base=0, channel_multiplier=1),
             reads=['zerof'], writes=['maskb_prev'])
        S.op('pool', lambda e: e.affine_select(mask_ut[:], onesf[0:64, 0:384].rearrange("p (h q) -> p h q", q=64),
                                               [[0, 6], [1, 64]], ALU.is_ge, 0.0, base=0, channel_multiplier=-1),
             reads=['onesf'], writes=['mask_ut'])

        def ld_small(dst, src, key):
            S.dma('pool', 'ld_small', lambda e: e.dma_start(out=dst[:], in_=src), writes=[key])

        ld_small(norms, norms_d, 'norms')
        ld_small(fnorm, fnorm_d, 'fnorm')
        ld_small(lbl, lbl_d, 'lbl')
        ld_small(ogn, ogn_d, 'ogn')
        ld_small(esink, sinks_d, 'esink')
        ld_small(cw, cw_d, 'cw')
        ld_small(cpar, cp_d, 'cpar')
        allsmall = [('ld_small', S.cnt['ld_small'])]
        for k in ('norms', 'fnorm', 'lbl', 'ogn', 'esink', 'cw', 'cpar'):
            S.lastw[k] = allsmall[0]
        ts('pool', norms[:], norms[:], 32.0, None, ALU.mult, None, ['norms'], ['norms'])
        ts('pool', fnorm[:], fnorm[:], 32.0, None, ALU.mult, None, ['fnorm'], ['fnorm'])
        ts('pool', ogn[:], ogn[:], 8.0, None, ALU.mult, None, ['ogn'], ['ogn'])
        act_op(esink[:], esink[:], AF.Exp, ['esink'], ['esink'])
        act_op(lbe[:], lbl[:], AF.Exp, ['lbl'], ['lbe'])
        S.op('dve', lambda e: e.tensor_reduce(lbs[:], lbe[:], AX.X, ALU.add), ['lbe'], ['lbs'])
        S.op('dve', lambda e: e.reciprocal(lbs[:], lbs[:]), ['lbs'], ['lbs'])
        tt('dve', lbe[:], lbe[:], lbs[:].unsqueeze(2).to_broadcast([64, 6, L]), ALU.mult, ['lbe', 'lbs'], ['lbe'])
        S.op('dve', lambda e: e.memset(lb[:], 0.0), [], ['lb'])
        for l in range(1, L):
            tt('dve', lb[:, :, l], lb[:, :, l - 1], lbe[:, :, l], ALU.add, ['lb', 'lbe'], ['lb'])
        ts('dve', oml[:], lb[:], -1.0, 1.0, ALU.mult, ALU.add, ['lb'], ['oml'])
        ts('dve', noml[:], lb[:], -1.0, None, ALU.add, None, ['lb'], ['noml'])
        for l in range(L):
            S.op('pool', lambda e, l=l: e.memset(Sst[l][:], 0.0), [], ['S%d' % l])
            S.op('pool', lambda e, l=l: e.memset(kext[l][:], 0.0), [], ['kext%d' % l])
            S.op('pool', lambda e, l=l: e.memset(vext[l][:], 0.0), [], ['vext%d' % l])
            S.op('pool', lambda e, l=l: e.memset(glu[l][:], 0.0), [], ['glu%d' % l])

        def cast(dst, src, l):
            S.dma('pool', 'cast%d' % l, lambda e: e.dma_start(out=dst, in_=src))

        for l in range(L):
            for i in range(2):
                for g in range(0, NF, 2):
                    cast(wgu_s[i][l, g:g + 2], wgu_d[i][l, g:g + 2], l)
                for g in range(0, 8, 2):
                    cast(wd_s[i][l, g:g + 2], wd_d[i][l, g:g + 2], l)
                if i == 0:
                    for g in range(7):
                        cast(win_s[l, g], win_d[l, g], l)
                    for g in range(0, 8, 4):
                        cast(wo_s[l, g:g + 4], wo_d[l, g:g + 4], l)
            S.lastw['wscr%d' % l] = ('cast%d' % l, S.cnt['cast%d' % l])

        tiles = ([(-1, NMETA)] if nmeta_tile else []) + [(i, TS) for i in range(NT)]
        p_wgu, p_wd, p_win, p_wo = [], [], [], []
        for (ti, T) in tiles:
            nsub = 1 if T <= TM else T // TM
            for l in range(L):
                rk = ['wscr%d' % l]
                for i in range(2):
                    for f in range(NF):
                        p_wgu.append((wgu_s[i][l, f], rk))
                    for d in range(8):
                        p_wd.append((wd_s[i][l, d], rk))
                    if i == 0:
                        for _ in range(nsub):
                            for g in range(7):
                                p_win.append((win_s[l, g], rk))
                            for d in range(8):
                                p_wo.append((wo_s[l, d], rk))
        st_wgu = Stream(S, 'wgu', wgu_slots, p_wgu, 2)
        st_wd = Stream(S, 'wd', wd_slots, p_wd, 2)
        st_win = Stream(S, 'win', win_slots, p_win, 3)
        st_wo = Stream(S, 'wo', wo_slots, p_wo, 2)
        for st in (st_wgu, st_wd, st_win, st_wo):
            st.prime()

        def rmsnorm(gcol, T, out_fn):
            sqv = act[:, 0:8, :T]
            act_op(sqv, hT[:, :, :T], AF.Square, HK, ACTK[0:8])
            ps, pk = PS()
            mm_group(ps[:, :T], [(ones_bf[:, :], act[:, kc, :T]) for kc in range(8)],
                     ['ones_bf'] + ACTK[0:8], [pk])
            rsq(rstd[:, :T], ps[:, :T], 1024.0 * EPS, [pk], ['rstd'])
            for kc in range(8):
                o, wk = out_fn(kc)
                eng = 'dve' if kc % 2 == 0 else 'pool'
                stt(eng, o, hT[:, kc, :T], gcol(kc), rstd[:, :T], ALU.mult, ALU.mult,
                    [HK[kc], 'rstd', 'norms', 'fnorm'], wk)

        def ffn(l, which, T):
            rmsnorm(lambda kc: norms[:, l, 2 * which, kc:kc + 1], T,
                    lambda kc: (xn[:, kc, :T], ['xn:%d' % kc]))
            XK = ['xn:%d' % kc for kc in range(8)]
            for f in range(NF):
                w, wk = st_wgu.get()
                psg, pgk = PS()
                mm_group(psg[:, :T], [(w[:, 0, kc, :], xn[:, kc, :T]) for kc in range(8)], [wk] + XK, [pgk])
                psu, puk = PS()
                mm_group(psu[:, :T], [(w[:, 1, kc, :], xn[:, kc, :T]) for kc in range(8)], [wk] + XK, [puk])
                st_wgu.after_use()
                sgt = sg[f % 2]
                sgk = 'sg%d' % (f % 2)
                act_op(sgt[:, :T], psg[:, :T], AF.Silu, [pgk], [sgk])
                tt('dve', act[:, f, :T], psu[:, :T], sgt[:, :T], ALU.mult, [puk, sgk], [ACTK[f]])
            for d in range(8):
                w, wk = st_wd.get()
                psd, pdk = PS()
                mm_group(psd[:, :T], [(w[:, f, :], act[:, f, :T]) for f in range(NF)], [wk] + ACTK, [pdk])
                st_wd.after_use()
                stt('dve', hT[:, d, :T], psd[:, :T], 0.5, hT[:, d, :T], ALU.mult, ALU.add, [pdk, HK[d]], [HK[d]])

        def mixer(l, c0, T, first_real):
            meta = (T == NMETA)
            C = NMETA if meta else 64
            nch = T // C
            mid = C // 2 - 1
            XK = ['xn:%d' % kc for kc in range(8)]

            def rms_out(kc):
                return xn[:, kc, :T], ['xn:%d' % kc]
            sqv = act[:, 0:8, :T]
            act_op(sqv, hT[:, :, c0:c0 + T], AF.Square, HK, ACTK[0:8])
            ps, pk = PS()
            mm_group(ps[:, :T], [(ones_bf[:, :], act[:, kc, :T]) for kc in range(8)], ['ones_bf'] + ACTK[0:8], [pk])
            rsq(rstd[:, :T], ps[:, :T], 1024.0 * EPS, [pk], ['rstd'])
            for kc in range(8):
                eng = 'dve' if kc % 2 == 0 else 'pool'
                stt(eng, xn[:, kc, :T], hT[:, kc, c0:c0 + T], norms[:, l, 1, kc:kc + 1], rstd[:, :T],
                    ALU.mult, ALU.mult, [HK[kc], 'rstd', 'norms'], ['xn:%d' % kc])

            wp = []
            for g in range(7):
                wp.append(None)

            def getpiece(g):
                if wp[g] is None:
                    wp[g] = st_win.get()
                return wp[g]

            def col(c):
                g = c // 384
                w, wk = getpiece(g)
                return w, wk, c - g * 384

            def proj_fm(c, M, T_):
                w, wk, lc = col(c)
                ps, pk = PS()
                mm_group(ps[0:M, :T_], [(w[:, kc, lc:lc + M], xn[:, kc, :T_]) for kc in range(8)], [wk] + XK, [pk])
                return ps, pk

            Av = lambda t: t[:, :, 0:nch, 0:C]
            lcol = lambda t, h: t[:, h, l:l + 1]

            for h in range(6):
                ps, pk = proj_fm(AQ + 64 * h, 64, T)
                cp('act' if h % 2 else 'dve', osb[:, h, 0:nch, 0:C],
                   ps[0:64, :T].rearrange("p (c s) -> p c s", s=C), [pk], ['osb'])
            st_win.after_use()
            for h in range(6):
                ps, pk = proj_fm(AFo + 64 * h, 64, T)
                act_op(bufB[:, h, 0:nch, 0:C], ps[0:64, :T].rearrange("p (c s) -> p c s", s=C), AF.Sigmoid,
                       [pk], ['bufB'])
            st_win.after_use()
            for h in range(6):
                ts('dve', bufA[:, h, 0:nch, 0:C], bufB[:, h, 0:nch, 0:C], lcol(oml, h), lcol(lb, h),
                   ALU.mult, ALU.add, ['bufB', 'oml', 'lb'], ['bufA'])
                ts('pool', bufB[:, h, 0:nch, 0:C], bufB[:, h, 0:nch, 0:C], lcol(noml, h), lcol(oml, h),
                   ALU.mult, ALU.add, ['bufB', 'bufA', 'noml', 'oml'], ['bufB'])
            act_op(Av(bufA), Av(bufA), AF.Ln, ['bufA'], ['bufA'])
            w2, wk2, _ = col(AI)
            for c in range(nch):
                ps, pk = PS()
                mm_group(ps[0:C, 0:384], [(xn[:, kc, c * C:(c + 1) * C], w2[:, kc, :]) for kc in range(8)],
                         [wk2] + XK, [pk])
                cp('act' if c % 2 else 'dve', vtokA[0:C, c, :], ps[0:C, 0:384], [pk], ['vtokA'])
            st_win.after_use()
            for h in range(6):
                ps, pk = proj_fm(AG + 64 * h, 64, T)
                act_op(sgate[:, h, 0:nch, 0:C], ps[0:64, :T].rearrange("p (c s) -> p c s", s=C), AF.Silu,
                       [pk], ['sgate'])
            st_win.after_use()
            for h in range(6):
                if meta:
                    o_ap, i_ap = bufC[:, h, 0, 0:C], bufA[:, h, 0, 0:C]
                else:
                    o_ap = bufC[:, h, :, :].rearrange("p c s -> p (c s)")
                    i_ap = bufA[:, h, :, :].rearrange("p c s -> p (c s)")
                S.op('dve', lambda e, o_ap=o_ap, i_ap=i_ap: e.tensor_tensor_scan(
                    o_ap, onesf[0:64, 0:T], i_ap, 0.0, ALU.mult, ALU.add), ['bufA', 'onesf'], ['bufC'])
            S.op('pool', lambda e: e.memset(GS[:], 0.0), [], ['GS'])
            if nch > 1:
                cp('pool', GS[:, :, 1:nch], bufC[:, :, 0:nch - 1, C - 1], ['bufC', 'GS'], ['GS'])
            tt('pool', dd[:, :, 0, 0:nch], bufC[:, :, 0:nch, mid], GS[:, :, 0:nch], ALU.subtract, ['bufC', 'GS'], ['dd'])
            tt('pool', dd[:, :, 1, 0:nch], bufC[:, :, 0:nch, C - 1], GS[:, :, 0:nch], ALU.subtract, ['bufC', 'GS', 'dd'], ['dd'])
            tt('pool', dd[:, :, 2, 0:nch], bufC[:, :, 0:nch, C - 1], bufC[:, :, 0:nch, mid], ALU.subtract, ['bufC', 'dd'], ['dd'])
            act_op(ESC[:, :, :, 0:nch], dd[:, :, :, 0:nch], AF.Exp, ['dd'], ['ESC'])
            tt('dve', Av(bufA), Av(bufC), bufC[:, :, 0:nch, mid:mid + 1].to_broadcast([64, 6, nch, C]), ALU.subtract,
               ['bufC', 'bufA'], ['bufA'])
            act_op(Av(bufC), Av(bufA), AF.Exp, ['bufA', 'bufC'], ['bufC'], scale=-1.0)
            tt('pool', Av(kt), Av(bufB), Av(bufC), ALU.mult, ['bufB', 'bufC'], ['kt'])
            act_op(Av(bufC), Av(bufA), AF.Exp, ['bufA', 'bufC', 'kt'], ['bufC'])
            tt('dve', Av(qt), Av(osb), Av(bufC), ALU.mult, ['osb', 'bufC'], ['qt'])

            for h in range(6):
                ps, pk = proj_fm(BQ + 64 * h, 64, T)
                ts('dve', qb[:, h, 0:T], ps[0:64, :T], 0.125, None, ALU.mult, None, [pk], ['qb'])
            st_win.after_use()
            KX, VX = 'kext%d' % l, 'vext%d' % l
            for g2 in range(2):
                ps, pk = proj_fm(BK + 64 * g2, 64, T)
                if meta:
                    cp('act', kmeta[l][:, g2, :], ps[0:64, :T], [pk], ['kmeta%d' % l])
                else:
                    cp('act', kext[l][:, g2, 128:128 + T], ps[0:64, :T], [pk, KX], [KX])
            w5, wk5, lc5 = col(BV)
            if meta:
                ps, pk = PS()
                mm_group(ps[0:T, 0:128], [(xn[:, kc, 0:T], w5[:, kc, lc5:lc5 + 128]) for kc in range(8)], [wk5] + XK, [pk])
                cp('dve', vmeta[l][:, :], ps[0:T, 0:128], [pk], ['vmeta%d' % l])
            else:
                for b in range(T // 128):
                    ps, pk = PS()
                    mm_group(ps[:, 0:128], [(xn[:, kc, b * 128:(b + 1) * 128], w5[:, kc, lc5:lc5 + 128]) for kc in range(8)],
                             [wk5] + XK, [pk])
                    cp('dve' if b % 2 else 'act', vext[l][:, 1 + b, :], ps[:, 0:128], [pk, VX], [VX])
            GK = 'glu%d' % l
            for c in range(2):
                psa, pka = proj_fm(CU + 128 * c, 128, T)
                psb_, pkb = proj_fm(CU + 256 + 128 * c, 128, T)
                act_op(sgm[c][:, :T], psb_[:, :T], AF.Sigmoid, [pkb], ['sgm%d' % c])
                tt('dve', glu[l][:, c, 30:30 + T], psa[:, :T], sgm[c][:, :T], ALU.mult, [pka, 'sgm%d' % c, GK], [GK])
            st_win.after_use()
            st_win.after_use()

            for c in range(2):
                ck = 'cacc%d' % c
                ts('dve', cacc[:, c, :T], glu[l][:, c, 0:T], cw[:, l, c, 0:1], cpar[:, l, 0, c:c + 1],
                   ALU.mult, ALU.add, [GK, 'cw', 'cpar'], [ck])
                for j in range(1, 31):
                    stt('pool', cacc[:, c, :T], glu[l][:, c, j:j + T], cw[:, l, c, j:j + 1], cacc[:, c, :T],
                        ALU.mult, ALU.add, [GK, 'cw', ck], [ck])
                cp('pool', cvb[:, c, :T], cacc[:, c, :T], [ck], ['cvb%d' % c])
                act_op(sqb[:, c, :T], cacc[:, c, :T], AF.Square, [ck], ['sqb%d' % c])

            SK = 'S%d' % l
            ya_writes = []
            for c in range(nch):
                pst, ptk = PS()
                pst_bf = pst[:, :].bitcast(BF16)

                def tr(pe, c=c, pst_bf=pst_bf):
                    ins = None
                    for h in range(6):
                        ins = pe.transpose(pst_bf[0:C, h * 64:(h + 1) * 64], kt[0:64, h, c, 0:C], ident_bf[0:64, 0:64])
                    return ins
                S.op('pe', tr, ['kt', 'ident_bf'], [ptk])
                kto = ktok[c % 2]
                kk_ = 'ktok%d' % (c % 2)
                cp('act', kto[0:C, :], pst_bf[0:C, 0:384], [ptk], [kk_])
                psa_, pak = PS()
                mm_multi([(psa_[0:C, h * C:(h + 1) * C], [(kt[0:64, h, c, 0:C], qt[0:64, h, c, 0:C])]) for h in range(6)],
                         ['kt', 'qt'], [pak])
                ats = at_sb[c % 2]
                atk = 'at_sb%d' % (c % 2)
                tt('dve', ats[0:C, 0:6 * C].rearrange("p (h t) -> p h t", t=C),
                   psa_[0:C, 0:6 * C].rearrange("p (h t) -> p h t", t=C), mask_ut[0:C, :, 0:C], ALU.mult,
                   [pak, 'mask_ut'], [atk])
                tt('dve', Sr[:], Sst[l][:], ESC[:, :, 0, c].unsqueeze(2).to_broadcast([64, 6, 64]), ALU.mult,
                   [SK, 'ESC'], ['Sr'])
                pso, pok = PS()
                mm_multi([(pso[0:64, h * C:(h + 1) * C],
                           [(vtokA[0:C, c, h * 64:(h + 1) * 64], ats[0:C, h * C:(h + 1) * C]),
                            (Sr[0:64, h, :], qt[0:64, h, c, 0:C])]) for h in range(6)],
                         ['vtokA', atk, 'Sr', 'qt'], [pok])
                cp('act', osb[:, :, c, 0:C], pso[0:64, 0:6 * C].rearrange("p (h t) -> p h t", t=C), [pok, 'qt', 'osb'], ['osb'])
                psu_, puk = PS()
                mm_multi([(psu_[0:64, h * 64:(h + 1) * 64],
                           [(kto[0:C, h * 64:(h + 1) * 64], vtokA[0:C, c, h * 64:(h + 1) * 64])]) for h in range(6)],
                         [kk_, 'vtokA'], [puk])
                tt('dve', t1[:], psu_[0:64, 0:384].rearrange("p (h v) -> p h v", v=64),
                   ESC[:, :, 2, c].unsqueeze(2).to_broadcast([64, 6, 64]), ALU.mult, [puk, 'ESC'], ['t1'])
                tt('pool', Sst[l][:], Sst[l][:], ESC[:, :, 1, c].unsqueeze(2).to_broadcast([64, 6, 64]), ALU.mult,
                   [SK, 'ESC'], [SK])
                tt('pool', Sst[l][:], Sst[l][:], t1[:], ALU.add, [SK, 't1'], [SK])
            OK2 = ['osb']
            act_op(Av(sqa), Av(osb), AF.Square, OK2, ['sqa'])
            for h in range(6):
                ps, pk = PS()
                if meta:
                    rhs = sqa[0:64, h, 0, 0:C]
                else:
                    rhs = sqa[0:64, h, :, :].rearrange("p c s -> p (c s)")
                mm_group(ps[0:64, :T], [(ones_bf[0:64, 0:64], rhs)], ['ones_bf', 'sqa'], [pk])
                r = rsa[h % 2]
                rk_ = 'rsa%d' % (h % 2)
                rsq(r[:, :T], ps[0:64, :T], 64.0 * EPS, [pk], [rk_])
                if meta:
                    o3, r3, g3, y3 = osb[:, h, 0, 0:C], r[:, :T], sgate[:, h, 0, 0:C], ya[:, h, 0, 0:C]
                else:
                    o3 = osb[:, h, :, :].rearrange("p c s -> p (c s)")
                    g3 = sgate[:, h, :, :].rearrange("p c s -> p (c s)")
                    y3 = ya[:, h, :, :].rearrange("p c s -> p (c s)")
                    r3 = r[:, :T]
                stt('dve', o3, o3, ogn[:, l:l + 1], r3, ALU.mult, ALU.mult, OK2 + [rk_, 'ogn', 'sqa'], ['osb'])
                tt('pool', y3, o3, g3, ALU.mult, ['osb', 'sgate'], ['ya:%d' % h])

            nblk = 1 if meta else T // 128
            QB = NMETA if meta else 128
            for b in range(nblk):
                for g2 in range(2):
                    i2 = (2 * b + g2) % 2
                    rq = qb[0:64, 3 * g2:3 * g2 + 3, b * QB:(b + 1) * QB]
                    srcs = []
                    if meta:
                        kc_ap, vc_ap = kmeta[l][:, g2, :], vmeta[l][:, g2 * 64:(g2 + 1) * 64]
                        kkeys, vkeys = ['kmeta%d' % l], ['vmeta%d' % l]
                    else:
                        kc_ap = kext[l][:, g2, 128 + b * 128:128 + (b + 1) * 128]
                        vc_ap = vext[l][:, 1 + b, g2 * 64:(g2 + 1) * 64]
                        kkeys, vkeys = [KX], [VX]
                    ps, pk = PS()
                    o_ = ps[0:QB, 0:3 * QB].rearrange("p (h q) -> p h q", q=QB)
                    mm_group(o_, [(kc_ap, rq), (ident_bf[0:QB, 0:QB], maskb_cur[0:QB, :, 0:QB])],
                             kkeys + ['qb', 'ident_bf', 'maskb_cur'], [pk])
                    pc_ = pcur[i2]
                    act_op(pc_[0:QB, 0:3 * QB], ps[0:QB, 0:3 * QB], AF.Exp, [pk], ['pcur%d' % i2])
                    srcs.append((pc_[0:QB, 0:3 * QB], 'pcur%d' % i2, vc_ap, QB))
                    if not meta:
                        if not (first_real and b == 0):
                            kp_ap = kext[l][:, g2, b * 128:(b + 1) * 128]
                            vp_ap = vext[l][:, b, g2 * 64:(g2 + 1) * 64]
                            ps, pk = PS()
                            o_ = ps[:, 0:384].rearrange("p (h q) -> p h q", q=128)
                            mm_group(o_, [(kp_ap, rq), (ident_bf[:, :], maskb_prev[:, :, :])],
                                     [KX, 'qb', 'ident_bf', 'maskb_prev'], [pk])
                            pp_ = pprev[i2]
                            act_op(pp_[:, :], ps[:, 0:384], AF.Exp, [pk], ['pprev%d' % i2])
                            srcs.append((pp_[:, :], 'pprev%d' % i2, vp_ap, 128))
                        ps, pk = PS()
                        o_ = ps[0:NMETA, 0:384].rearrange("p (h q) -> p h q", q=128)
                        mm_group(o_, [(kmeta[l][:, g2, :], rq)], ['kmeta%d' % l, 'qb'], [pk])
                        pm_ = pmeta[i2]
                        act_op(pm_[:, :], ps[0:NMETA, 0:384], AF.Exp, [pk], ['pmeta%d' % i2])
                        srcs.append((pm_[:, :], 'pmeta%d' % i2, vmeta[l][:, g2 * 64:(g2 + 1) * 64], NMETA))
                    pso, pok = PS()
                    mm_group(pso[0:64, 0:3 * QB], [(v_ap, p_ap) for (p_ap, _, v_ap, nk) in srcs],
                             [k for (_, k, _, _) in srcs] + vkeys + ['vmeta%d' % l], [pok])
                    psd, pdk = PS()
                    mm_group(psd[0:64, 0:3 * QB], [(ones_bf[0:nk, 0:64], p_ap) for (p_ap, _, v_ap, nk) in srcs],
                             [k for (_, k, _, _) in srcs] + ['ones_bf'], [pdk])
                    tt('dve', den[:, :, 0:QB], psd[0:64, 0:3 * QB].rearrange("p (h q) -> p h q", q=QB),
                       esink[:, l, 3 * g2:3 * g2 + 3].unsqueeze(2).to_broadcast([64, 3, QB]), ALU.add,
                       [pdk, 'esink'], ['den'])
                    S.op('dve', lambda e, QB=QB: e.reciprocal(rden[:, :, 0:QB], den[:, :, 0:QB]), ['den'], ['rden'])
                    tt('dve', yb[:, 3 * g2:3 * g2 + 3, b * QB:(b + 1) * QB],
                       pso[0:64, 0:3 * QB].rearrange("p (h q) -> p h q", q=QB), rden[:, :, 0:QB], ALU.mult,
                       [pok, 'rden'], ['yb'])

            psm, pmk = PS()
            mm_group(psm[:, :T], [(ones_bf[:, :], cvb[:, c, :T]) for c in range(2)], ['ones_bf', 'cvb0', 'cvb1'], [pmk])
            pss, psk = PS()
            mm_group(pss[:, :T], [(ones_bf[:, :], sqb[:, c, :T]) for c in range(2)], ['ones_bf', 'sqb0', 'sqb1'], [psk])
            ts('dve', cm[:, :T], psm[:, :T], 1.0 / 256.0, None, ALU.mult, None, [pmk], ['cm'])
            tt('pool', cmsq[:, :T], cm[:, :T], cm[:, :T], ALU.mult, ['cm'], ['cmsq'])
            stt('dve', cvar[:, :T], pss[:, :T], 1.0 / 256.0, cmsq[:, :T], ALU.mult, ALU.subtract, [psk, 'cmsq'], ['cvar'])
            rsq(crs[:, :T], cvar[:, :T], EPS, ['cvar'], ['crs'])
            for c in range(2):
                dk = 'dcen%d' % c
                tt('pool', dcen[c][:, :T], cacc[:, c, :T], cm[:, :T], ALU.subtract, ['cacc%d' % c, 'cm'], [dk])
                tt('pool', dcen[c][:, :T], dcen[c][:, :T], crs[:, :T], ALU.mult, [dk, 'crs'], [dk])
                act_op(yc[:, c, :T], dcen[c][:, :T], AF.Silu, [dk, 'cpar'], ['yc%d' % c],
                       bias=cpar[:, l, 2, c:c + 1], scale=cpar[:, l, 1, c:c + 1])

            YK = ['ya:%d' % h for h in range(6)] + ['yb', 'yc0', 'yc1']
            for d in range(8):
                w, wk = st_wo.get()
                ps, pk = PS()
                pairs = []
                for h in range(6):
                    if meta:
                        r_ = ya[0:64, h, 0, 0:C]
                    else:
                        r_ = ya[0:64, h, :, :].rearrange("p c s -> p (c s)")
                    pairs.append((w[0:64, h, :], r_))
                for h in range(6):
                    pairs.append((w[0:64, 6 + h, :], yb[0:64, h, 0:T]))
                for c in range(2):
                    pairs.append((w[:, 12 + c, :], yc[:, c, :T]))
                mm_group(ps[:, :T], pairs, [wk] + YK, [pk])
                st_wo.after_use()
                tt('dve', hT[:, d, c0:c0 + T], ps[:, :T], hT[:, d, c0:c0 + T], ALU.add, [pk, HK[d]], [HK[d]])

            if meta:
                cp('pool', glu[l][:, :, 14:30], glu[l][:, :, 30:46], [GK], [GK])
            else:
                cp('pool', kext[l][:, :, 0:128], kext[l][:, :, T:T + 128], [KX], [KX])
                cp('pool', vext[l][:, 0, :], vext[l][:, T // 128, :], [VX], [VX])
                cp('pool', glu[l][:, :, 0:30], glu[l][:, :, T:T + 30], [GK], [GK])

        out_evs = []
        first_real_done = False
        for (ti, T) in tiles:
            if ti < 0:
                S.dma('pool', 'ld_h', lambda e: e.dma_start(out=hT[:, :, 0:NMETA], in_=metaT), writes=HK)
            else:
                S.dma('pool', 'ld_h', lambda e, ti=ti: e.dma_start(out=hT[:], in_=xT[ti]), writes=HK)
            for l in range(L):
                ffn(l, 0, T)
                if T == NMETA:
                    mixer(l, 0, T, False)
                else:
                    for sub in range(T // TM):
                        mixer(l, sub * TM, TM, (not first_real_done) and sub == 0)
                ffn(l, 1, T)
            if ti >= 0:
                first_real_done = True
                def out_fn(kc):
                    return obuf[kc % 2][:, :T], ['obuf%d' % (kc % 2)]
                sqv = act[:, 0:8, :T]
                act_op(sqv, hT[:, :, :T], AF.Square, HK, ACTK[0:8])
                ps, pk = PS()
                mm_group(ps[:, :T], [(ones_bf[:, :], act[:, kc, :T]) for kc in range(8)], ['ones_bf'] + ACTK[0:8], [pk])
                rsq(rstd[:, :T], ps[:, :T], 1024.0 * EPS, [pk], ['rstd'])
                for kc in range(8):
                    ob = obuf[kc % 2]
                    ok = 'obuf%d' % (kc % 2)
                    stt('dve', ob[:, :T], hT[:, kc, :T], fnorm[:, kc:kc + 1], rstd[:, :T], ALU.mult, ALU.mult,
                        [HK[kc], 'rstd', 'fnorm'], [ok])
                    ev = S.dma('pool', 'st_o%d' % (kc % 2),
                               lambda e, ti=ti, kc=kc, ob=ob: e.dma_start(out=yT[ti, :, kc, :], in_=ob[:, :T]),
                               reads=[ok])
                    out_evs.append(ev)
        S.wait_events('pool', out_evs)
        S.emit()
    return nc


def pack_weights(inp, L):
    f = np.float32
    out = {}
    for i, pre in ((1, 'ffn1'), (2, 'ffn2')):
        wg = np.asarray(inp[pre + '_w_gate'], f)[:L]
        wu = np.asarray(inp[pre + '_w_up'], f)[:L]
        wdn = np.asarray(inp[pre + '_w_down'], f)[:L]
        g = wg.reshape(L, 8, 128, NF, 128).transpose(0, 3, 2, 1, 4)
        u = wu.reshape(L, 8, 128, NF, 128).transpose(0, 3, 2, 1, 4)
        out['wgu%d' % i] = np.ascontiguousarray(np.stack([g, u], axis=3))
        out['wd%d' % i] = np.ascontiguousarray(
            wdn.reshape(L, NF, 128, 8, 128).transpose(0, 3, 2, 1, 4))
    win = np.asarray(inp['w_in'], f)[:L]
    out['win'] = np.ascontiguousarray(win.reshape(L, 8, 128, 7, 384).transpose(0, 3, 2, 1, 4))
    wo = np.asarray(inp['w_out'], f)[:L]
    wop = np.zeros((L, 8, 128, 14, 128), f)
    ab = wo[:, :768].reshape(L, 12, 64, 8, 128).transpose(0, 3, 2, 1, 4)
    wop[:, :, :64, :12, :] = ab
    cc = wo[:, 768:].reshape(L, 2, 128, 8, 128).transpose(0, 3, 2, 1, 4)
    wop[:, :, :, 12:, :] = cc
    out['wo'] = wop
    norms = np.stack([np.asarray(inp['ffn1_norm'], f)[:L], np.asarray(inp['mix_norm'], f)[:L],
                      np.asarray(inp['ffn2_norm'], f)[:L]], axis=1)
    out['norms'] = np.ascontiguousarray(norms.reshape(L, 3, 8, 128).transpose(3, 0, 1, 2))
    out['fnorm'] = np.ascontiguousarray(np.asarray(inp['final_norm'], f).reshape(8, 128).T)
    out['lbl'] = np.ascontiguousarray(np.asarray(inp['hgrn_lb_logits'], f)[:L].reshape(L, 6, 64).transpose(2, 1, 0))
    out['ogn'] = np.ascontiguousarray(np.asarray(inp['hgrn_out_norm'], f)[:L].T)
    out['sinks'] = np.ascontiguousarray(np.broadcast_to(np.asarray(inp['attn_sinks'], f)[:L][None], (64, L, 6)))
    cwv = np.asarray(inp['conv_dw_w'], f)[:L]
    out['cw'] = np.ascontiguousarray(cwv.reshape(L, 31, 2, 128).transpose(3, 0, 2, 1))
    cpv = np.stack([np.asarray(inp['conv_dw_b'], f)[:L], np.asarray(inp['conv_ln_g'], f)[:L],
                    np.asarray(inp['conv_ln_b'], f)[:L]], axis=1)
    out['cp'] = np.ascontiguousarray(cpv.reshape(L, 3, 2, 128).transpose(3, 0, 1, 2))
    out['metaT'] = np.ascontiguousarray(np.asarray(inp['meta_tokens'], f).reshape(NMETA, 8, 128).transpose(2, 1, 0))
    return out


def pack_x(xb, NT):
    return np.ascontiguousarray(np.asarray(xb, np.float32).reshape(NT, TS, 8, 128).transpose(0, 3, 2, 1))


def unpack_y(yT, NT):
    return np.ascontiguousarray(yT.transpose(0, 3, 2, 1)).reshape(NT * TS, D)


_NC_CACHE = {}


def run(inputs, L, NT, ncores, trace=False):
    key = (L, NT)
    if key not in _NC_CACHE:
        _NC_CACHE[key] = build(L, NT)
    nc = _NC_CACHE[key]
    shared = pack_weights(inputs, L)
    x = np.asarray(inputs['x'], np.float32)
    in_maps = []
    for b in range(ncores):
        m = dict(shared)
        m['xT'] = pack_x(x[b, :NT * TS], NT)
        in_maps.append(m)
    res = run_bass_kernel_spmd(nc, in_maps, core_ids=list(range(ncores)), trace=trace)
    out = np.stack([unpack_y(r['yT'], NT) for r in res.results], axis=0)
    return out, res


def kernel(**inputs):
    out, _ = run(inputs, 4, 8, 8)
    return out.astype(np.float32)
```
